# Optimizing a Trainium2 kernel written in Bass

```python
import jax
import jax.numpy as jnp
from jax import lax
import numpy as np

D_MODEL = 1024
BATCH = 2
SEQ = 16384
DEPTH = 4

N_MIXERS = 2
N_ATT_LAYERS = (DEPTH + 1) // N_MIXERS
N_HGRN_LAYERS = DEPTH // N_MIXERS
HEAD_DIM = 64
N_Q_HEADS = D_MODEL // HEAD_DIM
N_KV_HEADS = N_Q_HEADS // 4
GROUP = N_Q_HEADS // N_KV_HEADS
Q_DIM = N_Q_HEADS * HEAD_DIM
KV_DIM = N_KV_HEADS * HEAD_DIM
WINDOW = 128
ATT_BLOCK = 128
ROPE_DIM = HEAD_DIM // 4
ROPE_THETA = 500000.0
HGRN_EXPAND = 128
HGRN_HEADS = D_MODEL // HGRN_EXPAND
HGRN_KEY = HGRN_EXPAND
HGRN_VAL = D_MODEL // HGRN_HEADS
HGRN_KW = HGRN_HEADS * HGRN_KEY
HGRN_VW = HGRN_HEADS * HGRN_VAL
HGRN_IN_DIM = 3 * HGRN_KW + 2 * HGRN_VW
HGRN_CHUNK = 64
D_FF = -(-8 * D_MODEL // (3 * 256)) * 256
PLE_DIM = 256
ALPHA = (2 * DEPTH) ** 0.25
BETA = (8 * DEPTH) ** -0.25
LN_EPS = 1e-5

kernel_name = 'hybrid_swa_hgrn2_deepnorm_encoder'


def layer_norm(x, g, b):
    xf = x.astype(jnp.float32)
    mu = xf.mean(-1, keepdims=True)
    var = jnp.square(xf - mu).mean(-1, keepdims=True)
    y = (xf - mu) * lax.rsqrt(var + LN_EPS) * g.astype(jnp.float32) + b.astype(jnp.float32)
    return y.astype(x.dtype)


def rope_tables(seq):
    inv = ROPE_THETA ** (-jnp.arange(0, ROPE_DIM, 2, dtype=jnp.float32) / ROPE_DIM)
    ang = jnp.arange(seq, dtype=jnp.float32)[:, None] * inv[None, :]
    return jnp.cos(ang), jnp.sin(ang)


def apply_partial_rope(x, cos, sin):
    half = ROPE_DIM // 2
    c = cos[None, :, None, :]
    s = sin[None, :, None, :]
    xr = x[..., :ROPE_DIM].astype(jnp.float32)
    x1, x2 = xr[..., :half], xr[..., half:]
    rot = jnp.concatenate([x1 * c - x2 * s, x2 * c + x1 * s], axis=-1).astype(x.dtype)
    return jnp.concatenate([rot, x[..., ROPE_DIM:]], axis=-1)


def windowed_gqa(h, w_qkv, sink, w_o, cos, sin):
    B, S, _ = h.shape
    qkv = h @ w_qkv
    q = qkv[..., :Q_DIM].reshape(B, S, N_Q_HEADS, HEAD_DIM)
    k = qkv[..., Q_DIM:Q_DIM + KV_DIM].reshape(B, S, N_KV_HEADS, HEAD_DIM)
    v = qkv[..., Q_DIM + KV_DIM:].reshape(B, S, N_KV_HEADS, HEAD_DIM)
    q = apply_partial_rope(q, cos, sin).reshape(B, S, N_KV_HEADS, GROUP, HEAD_DIM)
    k = apply_partial_rope(k, cos, sin)
    pad = ((0, 0), (ATT_BLOCK, ATT_BLOCK), (0, 0), (0, 0))
    kp = jnp.pad(k, pad)
    vp = jnp.pad(v, pad)
    n_blocks = S // ATT_BLOCK
    scale = HEAD_DIM ** -0.5
    sink_f = sink.astype(jnp.float32).reshape(N_KV_HEADS, GROUP)[None, :, :, None, None]

    def block_attn(bi):
        start = bi * ATT_BLOCK
        qb = lax.dynamic_slice_in_dim(q, start, ATT_BLOCK, axis=1).astype(jnp.float32)
        kb = lax.dynamic_slice_in_dim(kp, start, 3 * ATT_BLOCK, axis=1).astype(jnp.float32)
        vb = lax.dynamic_slice_in_dim(vp, start, 3 * ATT_BLOCK, axis=1)
        s = jnp.einsum('bqhgd,bkhd->bhgqk', qb, kb) * scale
        qpos = start + jnp.arange(ATT_BLOCK)
        kpos = start - ATT_BLOCK + jnp.arange(3 * ATT_BLOCK)
        valid = (jnp.abs(qpos[:, None] - kpos[None, :]) <= WINDOW) & (kpos >= 0)[None, :] & (kpos < S)[None, :]
        s = jnp.where(valid, s, -jnp.inf)
        m = jnp.maximum(s.max(-1, keepdims=True), sink_f)
        pr = jnp.exp(s - m)
        den = pr.sum(-1, keepdims=True) + jnp.exp(sink_f - m)
        return jnp.einsum('bhgqk,bkhd->bqhgd', (pr / den).astype(vb.dtype), vb)

    o = lax.map(block_attn, jnp.arange(n_blocks))
    o = o.transpose(1, 0, 2, 3, 4, 5).reshape(B, S, Q_DIM)
    return o @ w_o


def gated_linear_scan(q, k, v, log_f):
    B, S, H, K = q.shape
    V = v.shape[-1]
    nc = S // HGRN_CHUNK

    def to_chunks(t):
        return t.reshape(B, nc, HGRN_CHUNK, H, t.shape[-1]).transpose(1, 0, 3, 2, 4)

    lower = jnp.tril(jnp.ones((HGRN_CHUNK, HGRN_CHUNK), dtype=bool))

    def step(state, inp):
        qc, kc, vc, gc = inp
        b = jnp.cumsum(gc, axis=2)
        inter = jnp.einsum('bhtk,bhkv->bhtv', qc * jnp.exp(b), state)
        diff = b[:, :, :, None, :] - b[:, :, None, :, :]
        decay = jnp.exp(jnp.where(lower[:, :, None], diff, -jnp.inf))
        scores = jnp.einsum('bhtk,bhsk,bhtsk->bhts', qc, kc, decay)
        intra = jnp.einsum('bhts,bhsv->bhtv', scores, vc)
        b_last = b[:, :, -1:, :]
        new_state = jnp.exp(b_last[:, :, 0, :])[..., None] * state + jnp.einsum('bhsk,bhsv->bhkv', kc * jnp.exp(b_last - b), vc)
        return new_state, inter + intra

    state0 = jnp.zeros((B, H, K, V), q.dtype)
    _, out = lax.scan(step, state0, (to_chunks(q), to_chunks(k), to_chunks(v), to_chunks(log_f)))
    return out.transpose(1, 0, 3, 2, 4).reshape(B, S, H, V)


def hgrn2_bidirectional(h, w_in, lb_fwd, lb_bwd, norm_g, w_o):
    B, S, _ = h.shape
    z = h @ w_in
    o1, o2, o3, o4 = HGRN_KW, 2 * HGRN_KW, 3 * HGRN_KW, 3 * HGRN_KW + HGRN_VW
    q = jax.nn.silu(z[..., :o1].astype(jnp.float32)).reshape(B, S, HGRN_HEADS, HGRN_KEY)
    v = z[..., o3:o4].astype(jnp.float32).reshape(B, S, HGRN_HEADS, HGRN_VAL)
    gate = z[..., o4:].astype(jnp.float32).reshape(B, S, HGRN_HEADS, HGRN_VAL)

    def forget(zf, lb):
        sig = jax.nn.sigmoid(zf.astype(jnp.float32))
        f = lb + (1.0 - lb) * sig
        k = (1.0 - lb) * (1.0 - sig)
        shape = (B, S, HGRN_HEADS, HGRN_KEY)
        return jnp.log(f).reshape(shape), k.reshape(shape)

    logf_f, k_f = forget(z[..., o1:o2], lb_fwd)
    logf_b, k_b = forget(z[..., o2:o3], lb_bwd)
    out_f = gated_linear_scan(q, k_f, v, logf_f)
    flip = lambda t: jnp.flip(t, axis=1)
    out_b = flip(gated_linear_scan(flip(q), flip(k_b), flip(v), flip(logf_b)))
    o = out_f + out_b
    o = o * lax.rsqrt(jnp.mean(o * o, axis=-1, keepdims=True) + LN_EPS) * norm_g.astype(jnp.float32)
    o = o * jax.nn.silu(gate)
    return o.reshape(B, S, HGRN_VW).astype(h.dtype) @ w_o


def swiglu(x, w_in, w_out):
    gu = x @ w_in
    g, u = jnp.split(gu, 2, axis=-1)
    return (jax.nn.silu(g) * u) @ w_out


def setup_inputs(seed: int = 0) -> dict:
    key = jax.random.key(seed)
    ks = jax.random.split(key, 17)
    f32 = jnp.float32
    D = D_MODEL

    def nrm(k, shape, scale):
        return jax.random.normal(k, shape, f32) * scale

    return {
        'x': nrm(ks[0], (BATCH, SEQ, D), 1.0),
        'p': nrm(ks[1], (DEPTH, BATCH, SEQ, PLE_DIM), 1.0),
        'att_w_qkv': nrm(ks[2], (N_ATT_LAYERS, D, Q_DIM + 2 * KV_DIM), D ** -0.5),
        'att_sink': nrm(ks[3], (N_ATT_LAYERS, N_Q_HEADS), 0.5),
        'att_w_o': nrm(ks[4], (N_ATT_LAYERS, Q_DIM, D), BETA * Q_DIM ** -0.5),
        'hgrn_w_in': nrm(ks[5], (N_HGRN_LAYERS, D, HGRN_IN_DIM), D ** -0.5),
        'hgrn_lb_logits': nrm(ks[6], (DEPTH, 2, HGRN_KW), 0.1),
        'hgrn_norm_g': 1.0 + nrm(ks[7], (N_HGRN_LAYERS, HGRN_VAL), 0.01),
        'hgrn_w_o': nrm(ks[8], (N_HGRN_LAYERS, HGRN_VW, D), BETA * HGRN_VW ** -0.5),
        'ln_mix_g': 1.0 + nrm(ks[9], (DEPTH, D), 0.01),
        'ln_mix_b': nrm(ks[10], (DEPTH, D), 0.01),
        'ffn_w_in': nrm(ks[11], (DEPTH, D, 2 * D_FF), D ** -0.5),
        'ffn_w_out': nrm(ks[12], (DEPTH, D_FF, D), BETA * D_FF ** -0.5),
        'ln_ffn_g': 1.0 + nrm(ks[13], (DEPTH, D), 0.01),
        'ln_ffn_b': nrm(ks[14], (DEPTH, D), 0.01),
        'ple_w_gate': nrm(ks[15], (DEPTH, D, D), D ** -0.5),
        'ple_w_proj': nrm(ks[16], (DEPTH, PLE_DIM, D), BETA * PLE_DIM ** -0.5),
    }


def reference(x, p, att_w_qkv, att_sink, att_w_o, hgrn_w_in, hgrn_lb_logits, hgrn_norm_g, hgrn_w_o,
              ln_mix_g, ln_mix_b, ffn_w_in, ffn_w_out, ln_ffn_g, ln_ffn_b, ple_w_gate, ple_w_proj):
    S = x.shape[1]
    cos, sin = rope_tables(S)
    lb_sm = jax.nn.softmax(hgrn_lb_logits.astype(jnp.float32), axis=0)
    lb_all = jnp.cumsum(lb_sm, axis=0) - lb_sm[0:1]
    for i in range(DEPTH):
        j = i // N_MIXERS
        if i % N_MIXERS == 0:
            mix = windowed_gqa(x, att_w_qkv[j], att_sink[j], att_w_o[j], cos, sin)
        else:
            mix = hgrn2_bidirectional(x, hgrn_w_in[j], lb_all[i, 0], lb_all[i, 1], hgrn_norm_g[j], hgrn_w_o[j])
        x = layer_norm(ALPHA * x + mix, ln_mix_g[i], ln_mix_b[i])
        x = layer_norm(ALPHA * x + swiglu(x, ffn_w_in[i], ffn_w_out[i]), ln_ffn_g[i], ln_ffn_b[i])
        x = x + jax.nn.sigmoid(x @ ple_w_gate[i]) * (p[i] @ ple_w_proj[i])
    return x
```

```python
import numpy as np
from contextlib import ExitStack, contextmanager
import concourse.bass as bass
import concourse.mybir as mybir
from concourse.bass_utils import run_bass_kernel_spmd

F32 = mybir.dt.float32
BF16 = mybir.dt.bfloat16
AF = mybir.ActivationFunctionType
ALU = mybir.AluOpType
AX = mybir.AxisListType

D = 1024
SEQ = 16384
NCORE = 8
NTOK = 4096
DFF = 2816
NJ = DFF // 128
PLE = 256
ALPHA = 8.0 ** 0.25
EPS = 1e-5
NBLK = NTOK // 128
HBLK = NBLK + 2

SELF_SYNC = True
DEBUG = False


class Buf:
    def __init__(self, prog, name, t, is_dram=False):
        self.prog = prog
        self.name = name
        self.t = t
        self.is_dram = is_dram
        self.last_w = None
        self.reads = []
        self.dsem = None
        self.dcount = 0

    def __getitem__(self, key):
        return self.t[key]


class Prog:
    ENGS = ("pe", "act", "dve", "pool", "sp")

    def __init__(self, nc):
        self.nc = nc
        self.sem = {e: nc.alloc_semaphore("c_" + e) for e in ("pe", "act", "dve", "pool")}
        self.cnt = {e: 0 for e in self.sem}
        self.ops = {e: [] for e in self.ENGS}
        self.seen = {e: {} for e in self.ENGS}
        self.nbuf = 0
        self.all_dma_events = []
        self.banks = None
        self.bank_i = 0
        self.nops = 0
        self.stack = None
        self.phase_bufs = []
        self.free_dsems = []

    def sbuf(self, name, shape, dtype):
        self.uid = getattr(self, "uid", 0) + 1
        name = "%s_%d" % (name, self.uid)
        if self.stack is not None:
            t = self.stack.enter_context(self.nc.sbuf_tensor(name, list(shape), dtype))
            b = Buf(self, name, t)
            self.phase_bufs.append(b)
            return b
        t = self.nc.alloc_sbuf_tensor(name, list(shape), dtype)
        return Buf(self, name, t)

    @contextmanager
    def phase(self):
        self.stack = ExitStack()
        self.phase_bufs = []
        with self.stack:
            yield
            self.wait_all_dma("sp")
            self.emit()
        for b in self.phase_bufs:
            if b.dsem is not None:
                self.free_dsems.append((b.dsem, b.dcount))
                b.dsem = None
        self.phase_bufs = []
        self.stack = None

    def psum(self, name, shape=(128, 512), dtype=F32):
        t = self.nc.alloc_psum_tensor(name, list(shape), dtype)
        return Buf(self, name, t)

    def dram(self, name, shape, dtype, kind="Internal"):
        t = self.nc.dram_tensor(name, list(shape), dtype, kind=kind)
        return Buf(self, name, t.ap(), is_dram=True)

    def bank(self):
        if self.banks is None:
            self.banks = [self.psum("bank%d" % i) for i in range(8)]
        b = self.banks[self.bank_i % 8]
        self.bank_i += 1
        return b

    def _deps(self, eng, reads, writes, is_dma):
        evs = []
        for b in list(reads) + list(writes):
            if b.last_w is not None:
                evs.append(b.last_w)
        for b in writes:
            evs.extend(b.reads)
        need = {}
        for (s, v) in evs:
            if (not is_dma) and eng in self.sem and s is self.sem[eng]:
                if eng == "pe" or not SELF_SYNC:
                    continue
            k = id(s)
            if k not in need or need[k][1] < v:
                need[k] = (s, v)
        waits = []
        seen = self.seen[eng]
        for k, (s, v) in need.items():
            if seen.get(k, 0) >= v:
                continue
            seen[k] = v
            waits.append((s, v))
        return waits

    def _record(self, ev, reads, writes):
        for b in reads:
            b.reads.append(ev)
            if len(b.reads) > 64:
                best = {}
                for (s, v) in b.reads:
                    if id(s) not in best or best[id(s)][1] < v:
                        best[id(s)] = (s, v)
                b.reads = list(best.values())
        for b in writes:
            b.last_w = ev
            b.reads = []

    def op(self, eng, fn, reads=(), writes=()):
        waits = self._deps(eng, reads, writes, False)
        self.cnt[eng] += 1
        ev = (self.sem[eng], self.cnt[eng])
        self._record(ev, reads, writes)
        self.ops[eng].append((waits, fn, ev, 1))
        self.nops += 1

    def dma(self, queue, fn, reads=(), writes=(), owner=None):
        if owner is None:
            cands = [b for b in list(writes) + list(reads) if not b.is_dram]
            owner = cands[0] if cands else (list(writes) + list(reads))[0]
        waits = self._deps(queue, reads, writes, True)
        if owner.dsem is None:
            if self.free_dsems:
                owner.dsem, owner.dcount = self.free_dsems.pop()
            else:
                owner.dsem = self.nc.alloc_semaphore("d%d_%s" % (self.nbuf, owner.name))
                self.nbuf += 1
        owner.dcount += 1
        ev = (owner.dsem, 16 * owner.dcount)
        self._record(ev, reads, writes)
        self.ops[queue].append((waits, fn, ev, 16))
        self.all_dma_events.append(ev)
        self.nops += 1

    def wait_all_dma(self, eng="sp"):
        need = {}
        for (s, v) in self.all_dma_events:
            k = id(s)
            if k not in need or need[k][1] < v:
                need[k] = (s, v)
        waits = [(s, v) for k, (s, v) in need.items() if self.seen[eng].get(k, 0) < v]
        for (s, v) in waits:
            self.seen[eng][id(s)] = v
        self.ops[eng].append((waits, None, None, 0))

    def emit(self):
        nc = self.nc
        engs = {"pe": "tensor", "act": "scalar", "dve": "vector", "pool": "gpsimd", "sp": "sync"}
        with nc.Block() as block:
            def mk(ename):
                def body(e):
                    for (waits, fn, ev, inc) in self.ops[ename]:
                        for (s, v) in waits:
                            e.wait_ge(s, v)
                        if fn is None:
                            continue
                        inst = fn(e)
                        inst.then_inc(ev[0], inc)
                return body
            for en in self.ENGS:
                getattr(block, engs[en])(mk(en))
        for e in self.ENGS:
            self.ops[e] = []


class Common:
    def __init__(self, P, ident_d):
        self.P = P
        self.identf = P.sbuf("identf", [128, 128], F32)
        self.identb = P.sbuf("identb", [128, 128], BF16)
        self.mhalf = P.sbuf("mhalf", [128, 1], F32)
        P.dma("sp", lambda e: e.dma_start(out=self.identf[:], in_=ident_d[:]), reads=[ident_d], writes=[self.identf])
        P.dma("pool", lambda e: e.dma_start(out=self.identb[:], in_=ident_d[:]), reads=[ident_d], writes=[self.identb])
        P.op("dve", lambda e: e.memset(self.mhalf[:], -0.5), writes=[self.mhalf])
        self.ln_i = 0
        self.ybuf = [P.sbuf("ybuf%d" % i, [128, 1024], F32) for i in range(3)]
        self.st = [P.sbuf("lnst%d" % i, [128, 2, 6], F32) for i in range(3)]
        self.mv = [P.sbuf("lnmv%d" % i, [128, 2], F32) for i in range(3)]
        self.rstd = [P.sbuf("lnrs%d" % i, [128, 1], F32) for i in range(3)]


def load_w(P, dst, dst_ap, src, src_ap):
    P.dma("pool", lambda e: e.dma_start(out=dst_ap, in_=src_ap), reads=[src], writes=[dst])


def ln_part_a(P, C, banks, xres, xres_ap):
    s = C.ln_i % 3
    C.ln_i += 1
    y, st, mv, rstd = C.ybuf[s], C.st[s], C.mv[s], C.rstd[s]
    for nh in range(2):
        P.op("dve", lambda e, nh=nh: e.scalar_tensor_tensor(
            out=y[:, nh * 512:(nh + 1) * 512], in0=xres_ap[:, nh * 512:(nh + 1) * 512], scalar=ALPHA,
            in1=banks[nh][:, 0:512], op0=ALU.mult, op1=ALU.add), reads=[xres, banks[nh]], writes=[y])
        P.op("dve", lambda e, nh=nh: e.bn_stats(out=st[:, nh, :], in_=y[:, nh * 512:(nh + 1) * 512]),
             reads=[y], writes=[st])
    P.op("dve", lambda e: e.bn_aggr(out=mv[:], in_=st[:].rearrange("p a b -> p (a b)")), reads=[st], writes=[mv])
    P.op("dve", lambda e: e.tensor_scalar(out=rstd[:], in0=mv[:, 1:2], scalar1=EPS, scalar2=None, op0=ALU.add),
         reads=[mv], writes=[rstd])
    P.op("pool", lambda e: e.tensor_tensor(out=rstd[:], in0=rstd[:], in1=C.mhalf[:], op=ALU.pow),
         reads=[rstd, C.mhalf], writes=[rstd])
    return s


def ln_part_b(P, C, s, gbc, bbc, out, out_ap):
    y, mv, rstd = C.ybuf[s], C.mv[s], C.rstd[s]
    P.op("dve", lambda e: e.tensor_scalar(out=y[:], in0=y[:], scalar1=mv[:, 0:1], scalar2=rstd[:, 0:1],
                                          op0=ALU.subtract, op1=ALU.mult), reads=[y, mv, rstd], writes=[y])
    P.op("pool", lambda e: e.tensor_tensor(out=y[:], in0=y[:], in1=gbc[:], op=ALU.mult), reads=[y, gbc], writes=[y])
    P.op("pool", lambda e: e.tensor_tensor(out=out_ap, in0=y[:], in1=bbc[:], op=ALU.add), reads=[y, bbc], writes=[out])


def transpose_blocks(P, C, src, src_ap_fn, nb, nk, dst, dst_ap_fn, ident, alt=0):
    for k in range(nk):
        bk = P.bank()
        for b in range(nb):
            P.op("pe", lambda e, b=b, k=k, bk=bk: e.transpose(bk[:, b * 128:(b + 1) * 128], src_ap_fn(b, k), ident[:]),
                 reads=[src, ident], writes=[bk])
        if (k + alt) % 2 == 0:
            P.op("act", lambda e, k=k, bk=bk: e.copy(out=dst_ap_fn(k), in_=bk[:, 0:nb * 128]), reads=[bk], writes=[dst])
        else:
            P.op("dve", lambda e, k=k, bk=bk: e.tensor_copy(out=dst_ap_fn(k), in_=bk[:, 0:nb * 128]), reads=[bk], writes=[dst])


def emit_stage_c(P, C, W, l, xm_d, p_d, out_d, ntok=NTOK):
    TB = 512
    nb = TB // 128
    ntile = ntok // TB
    wout = P.sbuf("c_wout", [128, NJ, 1024], BF16)
    wg = P.sbuf("c_wg", [128, 8, 1024], BF16)
    wp = P.sbuf("c_wp", [128, 2, 1024], BF16)
    gbc = P.sbuf("c_gbc", [128, 1024], F32)
    bbc = P.sbuf("c_bbc", [128, 1024], F32)
    xmt = [P.sbuf("c_xmt%d" % i, [128, nb, 1024], F32) for i in range(2)]
    ptl = [P.sbuf("c_pt%d" % i, [128, nb, PLE], F32) for i in range(2)]
    xT = P.sbuf("c_xT", [128, 8, TB], BF16)
    pT = P.sbuf("c_pT", [128, 2, TB], BF16)
    hT = P.sbuf("c_hT", [128, NJ, TB], BF16)
    wi = [P.sbuf("c_wi%d" % i, [128, 8, 1024], BF16) for i in range(2)]
    sgt = [P.sbuf("c_sg%d" % i, [128, 512], F32) for i in range(3)]

    wo_src = W["ffn_w_out"]
    wo_v = wo_src[l].rearrange("(j p) n -> p j n", p=128)
    for j0 in range(0, NJ, 4):
        j1 = min(NJ, j0 + 4)
        load_w(P, wout, wout[:, j0:j1, :], wo_src, wo_v[:, j0:j1, :])
    wg_v = W["ple_w_gate"][l].rearrange("(k p) n -> p k n", p=128)
    for k0 in range(0, 8, 4):
        load_w(P, wg, wg[:, k0:k0 + 4, :], W["ple_w_gate"], wg_v[:, k0:k0 + 4, :])
    load_w(P, wp, wp[:], W["ple_w_proj"], W["ple_w_proj"][l].rearrange("(k p) n -> p k n", p=128))
    P.dma("sp", lambda e: e.dma_start(out=gbc[:], in_=W["ln_ffn_g"][l:l + 1, :].partition_broadcast(128)),
          reads=[W["ln_ffn_g"]], writes=[gbc])
    P.dma("sp", lambda e: e.dma_start(out=bbc[:], in_=W["ln_ffn_b"][l:l + 1, :].partition_broadcast(128)),
          reads=[W["ln_ffn_b"]], writes=[bbc])
    win_v = W["ffn_w_in"][l].rearrange("(k p) n -> p k n", p=128)
    wi_cnt = [0]
    sg_cnt = [0]

    for t in range(ntile):
        xm = xmt[t % 2]
        pt = ptl[t % 2]
        r0 = t * TB
        P.dma("sp", lambda e, xm=xm, r0=r0: e.dma_start(
            out=xm[:], in_=xm_d[r0:r0 + TB, :].rearrange("(b p) d -> p b d", p=128)), reads=[xm_d], writes=[xm])
        P.dma("sp", lambda e, pt=pt, r0=r0: e.dma_start(
            out=pt[:], in_=p_d[r0:r0 + TB, :].rearrange("(b p) d -> p b d", p=128)), reads=[p_d], writes=[pt])
        transpose_blocks(P, C, xm, lambda b, k, xm=xm: xm[:, b, k * 128:(k + 1) * 128], nb, 8,
                         xT, lambda k: xT[:, k, :], C.identf)
        transpose_blocks(P, C, pt, lambda b, k, pt=pt: pt[:, b, k * 128:(k + 1) * 128], nb, 2,
                         pT, lambda k: pT[:, k, :], C.identf)
        for j0 in range(0, NJ, 4):
            nj = min(4, NJ - j0)
            wb = wi[wi_cnt[0] % 2]
            wi_cnt[0] += 1
            load_w(P, wb, wb[:, :, 0:nj * 128], W["ffn_w_in"], win_v[:, :, j0 * 128:(j0 + nj) * 128])
            load_w(P, wb, wb[:, :, 512:512 + nj * 128], W["ffn_w_in"],
                   win_v[:, :, DFF + j0 * 128:DFF + (j0 + nj) * 128])
            for jj in range(nj):
                j = j0 + jj
                bg = P.bank()
                bu = P.bank()
                for k in range(8):
                    P.op("pe", lambda e, k=k, jj=jj, wb=wb, bg=bg: e.matmul(
                        bg[:, 0:TB], wb[:, k, jj * 128:(jj + 1) * 128], xT[:, k, :], start=(k == 0), stop=(k == 7)),
                        reads=[wb, xT], writes=[bg])
                for k in range(8):
                    P.op("pe", lambda e, k=k, jj=jj, wb=wb, bu=bu: e.matmul(
                        bu[:, 0:TB], wb[:, k, 512 + jj * 128:512 + (jj + 1) * 128], xT[:, k, :], start=(k == 0), stop=(k == 7)),
                        reads=[wb, xT], writes=[bu])
                sg = sgt[sg_cnt[0] % 3]
                sg_cnt[0] += 1
                P.op("act", lambda e, sg=sg, bg=bg: e.activation(out=sg[:], in_=bg[:, 0:TB], func=AF.Silu),
                     reads=[bg], writes=[sg])
                P.op("dve", lambda e, sg=sg, bu=bu, j=j: e.tensor_tensor(out=hT[:, j, :], in0=sg[:], in1=bu[:, 0:TB], op=ALU.mult),
                     reads=[sg, bu], writes=[hT])
        pend = None
        for b in range(nb + 1):
            if b < nb:
                bks = []
                for nh in range(2):
                    bk = P.bank()
                    for j in range(NJ):
                        P.op("pe", lambda e, j=j, b=b, nh=nh, bk=bk: e.matmul(
                            bk[:, 0:512], hT[:, j, b * 128:(b + 1) * 128], wout[:, j, nh * 512:(nh + 1) * 512],
                            start=(j == 0), stop=(j == NJ - 1)), reads=[hT, wout], writes=[bk])
                    bks.append(bk)
                s = ln_part_a(P, C, bks, xm, xm[:, b, :])
            if pend is not None:
                pb, ps = pend
                ln_part_b(P, C, ps, gbc, bbc, xm, xm[:, pb, :])
            pend = (b, s) if b < nb else None
        transpose_blocks(P, C, xm, lambda b, k, xm=xm: xm[:, b, k * 128:(k + 1) * 128], nb, 8,
                         xT, lambda k: xT[:, k, :], C.identf)
        for b in range(nb):
            for nh in range(2):
                bg = P.bank()
                bp = P.bank()
                for k in range(8):
                    P.op("pe", lambda e, k=k, b=b, nh=nh, bg=bg: e.matmul(
                        bg[:, 0:512], xT[:, k, b * 128:(b + 1) * 128], wg[:, k, nh * 512:(nh + 1) * 512],
                        start=(k == 0), stop=(k == 7)), reads=[xT, wg], writes=[bg])
                for k in range(2):
                    P.op("pe", lambda e, k=k, b=b, nh=nh, bp=bp: e.matmul(
                        bp[:, 0:512], pT[:, k, b * 128:(b + 1) * 128], wp[:, k, nh * 512:(nh + 1) * 512],
                        start=(k == 0), stop=(k == 1)), reads=[pT, wp], writes=[bp])
                sg = sgt[sg_cnt[0] % 3]
                sg_cnt[0] += 1
                P.op("act", lambda e, sg=sg, bg=bg: e.activation(out=sg[:], in_=bg[:, 0:512], func=AF.Sigmoid),
                     reads=[bg], writes=[sg])
                P.op("dve", lambda e, sg=sg, bp=bp: e.tensor_tensor(out=sg[:], in0=sg[:], in1=bp[:, 0:512], op=ALU.mult),
                     reads=[sg, bp], writes=[sg])
                P.op("pool", lambda e, sg=sg, b=b, nh=nh, xm=xm: e.tensor_tensor(
                    out=xm[:, b, nh * 512:(nh + 1) * 512], in0=sg[:], in1=xm[:, b, nh * 512:(nh + 1) * 512], op=ALU.add),
                    reads=[sg, xm], writes=[xm])
        P.dma("sp", lambda e, xm=xm, r0=r0: e.dma_start(
            out=out_d[r0:r0 + TB, :].rearrange("(b p) d -> p b d", p=128), in_=xm[:]), reads=[xm], writes=[out_d])


def emit_attn(P, C, W, l, j, xh_d, cs_d, mk_d, xm_d):
    wqkv = P.sbuf("a_wqkv", [128, 8, 1536], BF16)
    wo = P.sbuf("a_wo", [128, 8, 1024], BF16)
    gbc = P.sbuf("a_gbc", [128, 1024], F32)
    bbc = P.sbuf("a_bbc", [128, 1024], F32)
    cs = P.sbuf("a_cs", [128, HBLK, 16], F32)
    masks = P.sbuf("a_masks", [128, 4, 512], BF16)
    esink = P.sbuf("a_esink", [128, 16], F32)
    KT = P.sbuf("a_KT", [64, 4, HBLK * 128], BF16)
    V = P.sbuf("a_V", [128, HBLK, 4, 65], BF16)
    xb = [P.sbuf("a_xb%d" % i, [128, 1024], F32) for i in range(4)]
    xTb = P.sbuf("a_xTb", [128, 8, 128], BF16)
    qkvs = P.sbuf("a_qkvs", [128, 1536], F32)
    qr = P.sbuf("a_qr", [128, 16, 64], BF16)
    kr = P.sbuf("a_kr", [128, 4, 64], BF16)
    tmp = [P.sbuf("a_tmp%d" % i, [128, 16, 8], F32) for i in range(4)]
    qT = [P.sbuf("a_qT%d" % i, [64, 16, 128], BF16) for i in range(2)]
    es = [P.sbuf("a_es%d" % i, [128, 512], BF16) for i in range(6)]
    den = P.sbuf("a_den", [128, 4], F32)
    ob = P.sbuf("a_ob", [128, 16, 64], BF16)
    oT = P.sbuf("a_oT", [128, 8, 128], BF16)
    xo = [P.sbuf("a_xo%d" % i, [128, 1024], F32) for i in range(2)]

    wq_v = W["att_w_qkv"][j].rearrange("(k p) n -> p k n", p=128)
    for c in range(3):
        load_w(P, wqkv, wqkv[:, :, c * 512:(c + 1) * 512], W["att_w_qkv"], wq_v[:, :, c * 512:(c + 1) * 512])
    wo_v = W["att_w_o"][j].rearrange("(k p) n -> p k n", p=128)
    for k0 in range(0, 8, 4):
        load_w(P, wo, wo[:, k0:k0 + 4, :], W["att_w_o"], wo_v[:, k0:k0 + 4, :])
    P.dma("sp", lambda e: e.dma_start(out=gbc[:], in_=W["ln_mix_g"][l:l + 1, :].partition_broadcast(128)),
          reads=[W["ln_mix_g"]], writes=[gbc])
    P.dma("sp", lambda e: e.dma_start(out=bbc[:], in_=W["ln_mix_b"][l:l + 1, :].partition_broadcast(128)),
          reads=[W["ln_mix_b"]], writes=[bbc])
    P.dma("sp", lambda e: e.dma_start(out=cs[:], in_=cs_d[:, :].rearrange("(b p) c -> p b c", p=128)),
          reads=[cs_d], writes=[cs])
    P.dma("pool", lambda e: e.dma_start(out=masks[:], in_=mk_d[:, :, :].rearrange("m p c -> p m c")),
          reads=[mk_d], writes=[masks])
    P.dma("sp", lambda e: e.dma_start(out=esink[:], in_=W["att_sink"][j:j + 1, :].partition_broadcast(128)),
          reads=[W["att_sink"]], writes=[esink])
    P.op("act", lambda e: e.activation(out=esink[:], in_=esink[:], func=AF.Exp), reads=[esink], writes=[esink])
    P.op("dve", lambda e: e.memset(V[:].rearrange("p a b c -> p (a b c)"), 1.0), writes=[V])

    es_cnt = [0]
    pend = None
    for i in range(HBLK + 1):
        if i < HBLK:
            halo = (i == 0 or i == HBLK - 1)
            x_i = xb[i % 4]
            P.dma("sp", lambda e, x_i=x_i, i=i: e.dma_start(out=x_i[:], in_=xh_d[i * 128:(i + 1) * 128, :]),
                  reads=[xh_d], writes=[x_i])
            for kh in range(2):
                bk = P.bank()
                for kk in range(4):
                    k = kh * 4 + kk
                    P.op("pe", lambda e, k=k, kk=kk, bk=bk, x_i=x_i: e.transpose(
                        bk[:, kk * 128:(kk + 1) * 128], x_i[:, k * 128:(k + 1) * 128], C.identf[:]),
                        reads=[x_i, C.identf], writes=[bk])
                dst = xTb[:, kh * 4:(kh + 1) * 4, :].rearrange("p a b -> p (a b)")
                if kh == 0:
                    P.op("act", lambda e, bk=bk, dst=dst: e.copy(out=dst, in_=bk[:, 0:512]), reads=[bk], writes=[xTb])
                else:
                    P.op("dve", lambda e, bk=bk, dst=dst: e.tensor_copy(out=dst, in_=bk[:, 0:512]), reads=[bk], writes=[xTb])
            for c in ([2] if halo else [0, 1, 2]):
                bk = P.bank()
                for k in range(8):
                    P.op("pe", lambda e, k=k, c=c, bk=bk: e.matmul(
                        bk[:, 0:512], xTb[:, k, :], wqkv[:, k, c * 512:(c + 1) * 512], start=(k == 0), stop=(k == 7)),
                        reads=[xTb, wqkv], writes=[bk])
                P.op("act", lambda e, c=c, bk=bk: e.copy(out=qkvs[:, c * 512:(c + 1) * 512], in_=bk[:, 0:512]),
                     reads=[bk], writes=[qkvs])
            cosb = lambda nh, i=i: cs[:, i:i + 1, 0:8].to_broadcast([128, nh, 8])
            sinb = lambda nh, i=i: cs[:, i:i + 1, 8:16].to_broadcast([128, nh, 8])

            def rope(src3, dst, dst3, nh):
                x1 = src3[:, :, 0:8]
                x2 = src3[:, :, 8:16]
                t = [tt[:, 0:nh, :] for tt in tmp]
                cb = cosb(nh)
                sb = sinb(nh)
                P.op("dve", lambda e: e.tensor_tensor(out=t[0], in0=x1, in1=cb, op=ALU.mult), reads=[qkvs, cs], writes=[tmp[0]])
                P.op("dve", lambda e: e.tensor_tensor(out=t[1], in0=x2, in1=sb, op=ALU.mult), reads=[qkvs, cs], writes=[tmp[1]])
                P.op("dve", lambda e: e.tensor_tensor(out=t[2], in0=x2, in1=cb, op=ALU.mult), reads=[qkvs, cs], writes=[tmp[2]])
                P.op("dve", lambda e: e.tensor_tensor(out=t[3], in0=x1, in1=sb, op=ALU.mult), reads=[qkvs, cs], writes=[tmp[3]])
                P.op("dve", lambda e: e.tensor_tensor(out=dst3[:, :, 0:8], in0=t[0], in1=t[1], op=ALU.subtract),
                     reads=[tmp[0], tmp[1]], writes=[dst])
                P.op("dve", lambda e: e.tensor_tensor(out=dst3[:, :, 8:16], in0=t[2], in1=t[3], op=ALU.add),
                     reads=[tmp[2], tmp[3]], writes=[dst])
                P.op("pool", lambda e: e.tensor_copy(out=dst3[:, :, 16:64], in_=src3[:, :, 16:64]), reads=[qkvs], writes=[dst])

            if not halo:
                rope(qkvs[:, 0:1024].rearrange("p (h d) -> p h d", d=64), qr, qr[:], 16)
            rope(qkvs[:, 1024:1280].rearrange("p (h d) -> p h d", d=64), kr, kr[:], 4)
            P.op("pool", lambda e, i=i: e.tensor_copy(out=V[:, i, :, 0:64],
                                                     in_=qkvs[:, 1280:1536].rearrange("p (h d) -> p h d", d=64)),
                 reads=[qkvs], writes=[V])
            bk = P.bank()
            bkb = bk[:].bitcast(BF16)
            for h in range(4):
                P.op("pe", lambda e, h=h, bkb=bkb: e.transpose(bkb[0:64, h * 128:(h + 1) * 128], kr[:, h, :], C.identb[:]),
                     reads=[kr, C.identb], writes=[bk])
            P.op("act", lambda e, bkb=bkb, i=i: e.copy(out=KT[:, :, i * 128:(i + 1) * 128],
                                                      in_=bkb[0:64, 0:512].rearrange("p (h t) -> p h t", t=128)),
                 reads=[bk], writes=[KT])
            if not halo:
                qTi = qT[i % 2]
                for hh in range(2):
                    bk = P.bank()
                    bkb = bk[:].bitcast(BF16)
                    for h in range(8):
                        P.op("pe", lambda e, h=h, hh=hh, bkb=bkb: e.transpose(
                            bkb[0:64, h * 128:(h + 1) * 128], qr[:, hh * 8 + h, :], C.identb[:]),
                            reads=[qr, C.identb], writes=[bk])
                    dst = qTi[:, hh * 8:(hh + 1) * 8, :].rearrange("p h t -> p (h t)")
                    if hh == 0:
                        P.op("act", lambda e, bkb=bkb, dst=dst: e.copy(out=dst, in_=bkb[0:64, 0:1024]), reads=[bk], writes=[qTi])
                    else:
                        P.op("dve", lambda e, bkb=bkb, dst=dst: e.tensor_copy(out=dst, in_=bkb[0:64, 0:1024]), reads=[bk], writes=[qTi])
        qb = i - 1
        if 1 <= qb <= NBLK:
            qTq = qT[qb % 2]
            x_q = xb[qb % 4]
            for g in range(4):
                esg = []
                for kk, kb in enumerate((qb - 1, qb, qb + 1)):
                    bs = P.bank()
                    P.op("pe", lambda e, g=g, kb=kb, bs=bs, qTq=qTq: e.matmul(
                        bs[:, 0:512], KT[:, g, kb * 128:(kb + 1) * 128],
                        qTq[:, 4 * g:4 * g + 4, :].rearrange("p h t -> p (h t)"), start=True, stop=True),
                        reads=[KT, qTq], writes=[bs])
                    ee = es[es_cnt[0] % 6]
                    es_cnt[0] += 1
                    P.op("act", lambda e, bs=bs, ee=ee: e.activation(out=ee[:], in_=bs[:, 0:512], func=AF.Exp, scale=0.125),
                         reads=[bs], writes=[ee])
                    mi = None
                    if kk == 0:
                        mi = 2 if qb == 1 else 0
                    elif kk == 2:
                        mi = 3 if qb == NBLK else 1
                    if mi is not None:
                        P.op("pool", lambda e, ee=ee, mi=mi: e.tensor_tensor(out=ee[:], in0=ee[:], in1=masks[:, mi, :], op=ALU.mult),
                             reads=[ee, masks], writes=[ee])
                    esg.append((ee, kb))
                bo = P.bank()
                for hh in range(4):
                    for kk, (ee, kb) in enumerate(esg):
                        P.op("pe", lambda e, hh=hh, kk=kk, ee=ee, kb=kb, g=g, bo=bo: e.matmul(
                            bo[:, hh * 65:(hh + 1) * 65], ee[:, hh * 128:(hh + 1) * 128], V[:, kb, g, :],
                            start=(kk == 0), stop=(kk == 2)), reads=[ee, V], writes=[bo])
                bo3 = bo[:, 0:260].rearrange("p (h d) -> p h d", d=65)
                P.op("dve", lambda e, bo3=bo3, g=g: e.tensor_tensor(
                    out=den[:].rearrange("p (h o) -> p h o", o=1), in0=bo3[:, :, 64:65],
                    in1=esink[:, 4 * g:4 * g + 4].rearrange("p (h o) -> p h o", o=1), op=ALU.add),
                    reads=[bo, esink], writes=[den])
                P.op("dve", lambda e: e.reciprocal(out=den[:], in_=den[:]), reads=[den], writes=[den])
                P.op("dve", lambda e, bo3=bo3, g=g: e.tensor_tensor(
                    out=ob[:, 4 * g:4 * g + 4, :], in0=bo3[:, :, 0:64],
                    in1=den[:].rearrange("p (h o) -> p h o", o=1).to_broadcast([128, 4, 64]), op=ALU.mult),
                    reads=[bo, den], writes=[ob])
            bk = P.bank()
            bkb = bk[:].bitcast(BF16)
            obf = ob[:].rearrange("p h d -> p (h d)")
            for k in range(8):
                P.op("pe", lambda e, k=k, bkb=bkb: e.transpose(bkb[:, k * 128:(k + 1) * 128], obf[:, k * 128:(k + 1) * 128], C.identb[:]),
                     reads=[ob, C.identb], writes=[bk])
            P.op("act", lambda e, bkb=bkb: e.copy(out=oT[:].rearrange("p k t -> p (k t)"), in_=bkb[:, 0:1024]), reads=[bk], writes=[oT])
            bks = []
            for nh in range(2):
                bk = P.bank()
                for k in range(8):
                    P.op("pe", lambda e, k=k, nh=nh, bk=bk: e.matmul(
                        bk[:, 0:512], oT[:, k, :], wo[:, k, nh * 512:(nh + 1) * 512], start=(k == 0), stop=(k == 7)),
                        reads=[oT, wo], writes=[bk])
                bks.append(bk)
            s = ln_part_a(P, C, bks, x_q, x_q[:])
            if pend is not None:
                pqb, ps = pend
                xo_p = xo[pqb % 2]
                ln_part_b(P, C, ps, gbc, bbc, xo_p, xo_p[:])
                P.dma("sp", lambda e, xo_p=xo_p, pqb=pqb: e.dma_start(out=xm_d[(pqb - 1) * 128:pqb * 128, :], in_=xo_p[:]),
                      reads=[xo_p], writes=[xm_d])
            pend = (qb, s)
    pqb, ps = pend
    xo_p = xo[pqb % 2]
    ln_part_b(P, C, ps, gbc, bbc, xo_p, xo_p[:])
    P.dma("sp", lambda e: e.dma_start(out=xm_d[(pqb - 1) * 128:pqb * 128, :], in_=xo_p[:]), reads=[xo_p], writes=[xm_d])


WNAMES = ["att_w_qkv", "att_sink", "att_w_o", "hgrn_w_in", "hgrn_lb_logits", "hgrn_norm_g", "hgrn_w_o",
          "ln_mix_g", "ln_mix_b", "ffn_w_in", "ffn_w_out", "ln_ffn_g", "ln_ffn_b", "ple_w_gate", "ple_w_proj"]
WSHAPES = {
    "att_w_qkv": [2, 1024, 1536], "att_sink": [2, 16], "att_w_o": [2, 1024, 1024], "hgrn_w_in": [2, 1024, 5120],
    "hgrn_lb_logits": [4, 2, 1024], "hgrn_norm_g": [2, 128], "hgrn_w_o": [2, 1024, 1024], "ln_mix_g": [4, 1024],
    "ln_mix_b": [4, 1024], "ffn_w_in": [4, 1024, 5632], "ffn_w_out": [4, 2816, 1024], "ln_ffn_g": [4, 1024],
    "ln_ffn_b": [4, 1024], "ple_w_gate": [4, 1024, 1024], "ple_w_proj": [4, 256, 1024],
}


def declare_weights(P, names, sliced=True):
    out = {}
    for n in names:
        shp = list(WSHAPES[n])
        if sliced and n != "hgrn_lb_logits":
            shp[0] = 1
        out[n] = P.dram(n, shp, F32, kind="ExternalInput")
    return out


def slice_weights(Wnp, names, l):
    j = l // 2
    out = {}
    for n in names:
        if n == "hgrn_lb_logits":
            out[n] = Wnp[n]
        elif n.startswith("att_") or n in ("hgrn_w_in", "hgrn_norm_g", "hgrn_w_o"):
            out[n] = np.ascontiguousarray(Wnp[n][j:j + 1])
        else:
            out[n] = np.ascontiguousarray(Wnp[n][l:l + 1])
    return out


def build_attn_layer(l):
    nc = bass.Bass("TRN2", target_bir_lowering=False)
    P = Prog(nc)
    W = declare_weights(P, ["att_w_qkv", "att_sink", "att_w_o", "ln_mix_g", "ln_mix_b", "ffn_w_in", "ffn_w_out",
                            "ln_ffn_g", "ln_ffn_b", "ple_w_gate", "ple_w_proj"])
    ident_d = P.dram("ident", [128, 128], F32, kind="ExternalInput")
    xh_d = P.dram("xh", [HBLK * 128, D], F32, kind="ExternalInput")
    cs_d = P.dram("cs", [HBLK * 128, 16], F32, kind="ExternalInput")
    mk_d = P.dram("mk", [4, 128, 512], F32, kind="ExternalInput")
    p_d = P.dram("p", [NTOK, PLE], F32, kind="ExternalInput")
    xm_d = P.dram("xm_scr", [NTOK, D], F32, kind="ExternalOutput" if DEBUG else "Internal")
    out_d = P.dram("out", [NTOK, D], F32, kind="ExternalOutput")
    C = Common(P, ident_d)
    with P.phase():
        emit_attn(P, C, W, 0, 0, xh_d, cs_d, mk_d, xm_d)
    with P.phase():
        emit_stage_c(P, C, W, 0, xm_d, p_d, out_d)
    print("attn layer program ops:", P.nops)
    return nc


def rope_tables_np():
    inv = (500000.0 ** (-np.arange(0, 16, 2, dtype=np.float32) / 16)).astype(np.float32)
    ang = np.arange(SEQ, dtype=np.float32)[:, None] * inv[None, :]
    return np.cos(ang).astype(np.float32), np.sin(ang).astype(np.float32)


_COS_SIN = None


ATT_W = ["att_w_qkv", "att_sink", "att_w_o", "ln_mix_g", "ln_mix_b", "ffn_w_in", "ffn_w_out",
         "ln_ffn_g", "ln_ffn_b", "ple_w_gate", "ple_w_proj"]


def attn_in_maps(xfull, p_l, Wnp, l=0):
    global _COS_SIN
    wsl = slice_weights(Wnp, ATT_W, l)
    if _COS_SIN is None:
        _COS_SIN = rope_tables_np()
    cos, sin = _COS_SIN
    ident = np.eye(128, dtype=np.float32)
    pi = np.arange(128)[:, None]
    qi = np.arange(128)[None, :]
    mL = np.tile((pi >= qi).astype(np.float32), (1, 4))
    mR = np.tile((pi <= qi).astype(np.float32), (1, 4))
    maps = []
    for c in range(NCORE):
        b, jpos = c // 4, c % 4
        s0 = jpos * NTOK
        xh = np.zeros((HBLK * 128, D), np.float32)
        cs = np.zeros((HBLK * 128, 16), np.float32)
        lo, hi = max(0, s0 - 128), min(SEQ, s0 + NTOK + 128)
        off = lo - (s0 - 128)
        xh[off:off + hi - lo] = xfull[b, lo:hi]
        cs[off:off + hi - lo, 0:8] = cos[lo:hi]
        cs[off:off + hi - lo, 8:16] = sin[lo:hi]
        vl = 0.0 if jpos == 0 else 1.0
        vr = 0.0 if jpos == 3 else 1.0
        mk = np.stack([mL, mR, mL * vl, mR * vr]).astype(np.float32)
        m = {"ident": ident, "xh": xh, "cs": cs, "mk": mk,
             "p": np.ascontiguousarray(p_l[b, s0:s0 + NTOK])}
        m.update(wsl)
        maps.append(m)
    return maps


def gather_tokens(res, key="out"):
    out = np.zeros((2, SEQ, D), np.float32)
    for c in range(NCORE):
        b, jpos = c // 4, c % 4
        out[b, jpos * NTOK:(jpos + 1) * NTOK] = res.results[c][key]
    return out


HT = 256
HNB = HT // 128
HNC = HT // 64


def emit_lb(P, W, l, d, oml, noml):
    lg = P.sbuf("lb_lg", [128, 4, 8], F32)
    sm = P.sbuf("lb_sm", [128, 8], F32)
    acc = P.sbuf("lb_acc", [128, 8], F32)
    for ll in range(4):
        P.dma("sp", lambda e, ll=ll: e.dma_start(
            out=lg[:, ll, :], in_=W["hgrn_lb_logits"][ll, d, :].rearrange("(h k) -> k h", k=128),
            allow_slow_non_contiguous=True), reads=[W["hgrn_lb_logits"]], writes=[lg])
    P.op("act", lambda e: e.activation(out=lg[:], in_=lg[:], func=AF.Exp), reads=[lg], writes=[lg])
    P.op("dve", lambda e: e.tensor_tensor(out=sm[:], in0=lg[:, 0, :], in1=lg[:, 1, :], op=ALU.add), reads=[lg], writes=[sm])
    P.op("dve", lambda e: e.tensor_tensor(out=sm[:], in0=sm[:], in1=lg[:, 2, :], op=ALU.add), reads=[lg, sm], writes=[sm])
    P.op("dve", lambda e: e.tensor_tensor(out=sm[:], in0=sm[:], in1=lg[:, 3, :], op=ALU.add), reads=[lg, sm], writes=[sm])
    P.op("dve", lambda e: e.reciprocal(out=sm[:], in_=sm[:]), reads=[sm], writes=[sm])
    P.op("dve", lambda e: e.tensor_copy(out=acc[:], in_=lg[:, 1, :]), reads=[lg], writes=[acc])
    for ll in range(2, l + 1):
        P.op("dve", lambda e, ll=ll: e.tensor_tensor(out=acc[:], in0=acc[:], in1=lg[:, ll, :], op=ALU.add), reads=[lg, acc], writes=[acc])
    P.op("dve", lambda e: e.tensor_tensor(out=acc[:], in0=acc[:], in1=sm[:], op=ALU.mult), reads=[acc, sm], writes=[acc])
    P.op("dve", lambda e: e.tensor_scalar(out=noml[:], in0=acc[:], scalar1=-1.0, scalar2=None, op0=ALU.add), reads=[acc], writes=[noml])
    P.op("dve", lambda e: e.tensor_scalar(out=oml[:], in0=acc[:], scalar1=-1.0, scalar2=1.0, op0=ALU.mult, op1=ALU.add), reads=[acc], writes=[oml])


def emit_hgrn_sweep(P, C, W, l, j, d, want_out, x_d, cm_d, o_d, sin_d, sm_d, sout_d, aout_d, l_real=None):
    fwd = (d == 0)
    wzf = P.sbuf("h_wzf", [128, 8, 1024], BF16)
    wv = P.sbuf("h_wv", [128, 8, 1024], BF16)
    win_v = W["hgrn_w_in"][j].rearrange("(k p) n -> p k n", p=128)
    zoff = 1024 * (1 + d)
    for k0 in range(0, 8, 4):
        load_w(P, wzf, wzf[:, k0:k0 + 4, :], W["hgrn_w_in"], win_v[:, k0:k0 + 4, zoff:zoff + 1024])
        load_w(P, wv, wv[:, k0:k0 + 4, :], W["hgrn_w_in"], win_v[:, k0:k0 + 4, 3072:4096])
    if want_out:
        wq = P.sbuf("h_wq", [128, 8, 1024], BF16)
        for k0 in range(0, 8, 4):
            load_w(P, wq, wq[:, k0:k0 + 4, :], W["hgrn_w_in"], win_v[:, k0:k0 + 4, 0:1024])
    oml = P.sbuf("h_oml", [128, 8], F32)
    noml = P.sbuf("h_noml", [128, 8], F32)
    emit_lb(P, W, l if l_real is None else l_real, d, oml, noml)
    cm = P.sbuf("h_cm", [128, 3, 512], F32)
    P.dma("sp", lambda e: e.dma_start(out=cm[:], in_=cm_d[:, :, :].rearrange("m p c -> p m c")), reads=[cm_d], writes=[cm])
    rmask = cm[:, 0, 0:HT]
    tri = cm[:, 1 + d, :]
    xt = P.sbuf("h_xt", [128, HNB, 1024], F32)
    xT = P.sbuf("h_xT", [128, 8, HT], BF16)
    sgm = P.sbuf("h_sgm", [128, 8, HT], F32)
    gb = P.sbuf("h_gb", [128, 8, HT], F32)
    bb = P.sbuf("h_bb", [128, 8, HT], F32)
    kh = P.sbuf("h_kh", [128, 8, HT], BF16)
    kbT = P.sbuf("h_kbT", [128, 8, HT], BF16)
    kbm = P.sbuf("h_kbm", [128, HNB, 8, 128], BF16)
    vt = P.sbuf("h_vt", [128, HNB, 1024], BF16)
    S = P.sbuf("h_S", [128, 8, 128], F32)
    gsum = P.sbuf("h_gsum", [128, 8], F32)
    gred = P.sbuf("h_gred", [128, 8], F32)
    if want_out:
        sq = P.sbuf("h_sq", [128, 8, HT], F32)
        qt = P.sbuf("h_qt", [128, 8, HT], BF16)
        Sb = P.sbuf("h_Sb", [128, 8, 128], BF16)
        smk = [P.sbuf("h_smk%d" % i, [128, 512], BF16) for i in range(2)]
        osb = P.sbuf("h_osb", [128, HNB, 1024], F32)
        of = P.sbuf("h_of", [128, HNB, 1024], F32)
        emit_state_combine(P, d, sin_d, sm_d, S)
        P.op("act", lambda e: e.copy(out=Sb[:], in_=S[:]), reads=[S], writes=[Sb])
    else:
        P.op("dve", lambda e: e.memset(S[:].rearrange("p a b -> p (a b)"), 0.0), writes=[S])
    P.op("dve", lambda e: e.memset(gsum[:], 0.0), writes=[gsum])

    ntile = NTOK // HT
    order = list(range(ntile)) if fwd else list(range(ntile - 1, -1, -1))
    rv = (lambda ap: ap) if fwd else (lambda ap: ap[:, ::-1])
    smk_i = [0]
    for t in order:
        r0 = t * HT
        P.dma("sp", lambda e, r0=r0: e.dma_start(out=xt[:], in_=x_d[r0:r0 + HT, :].rearrange("(b p) d -> p b d", p=128)),
              reads=[x_d], writes=[xt])
        if want_out and not fwd:
            P.dma("sp", lambda e, r0=r0: e.dma_start(out=of[:], in_=o_d[r0:r0 + HT, :].rearrange("(b p) d -> p b d", p=128)),
                  reads=[o_d], writes=[of])
        transpose_blocks(P, C, xt, lambda b, k: xt[:, b, k * 128:(k + 1) * 128], HNB, 8, xT, lambda k: xT[:, k, :], C.identf)
        for h in range(8):
            bk = P.bank()
            for k in range(8):
                P.op("pe", lambda e, k=k, h=h, bk=bk: e.matmul(bk[:, 0:HT], wzf[:, k, h * 128:(h + 1) * 128], xT[:, k, :],
                                                              start=(k == 0), stop=(k == 7)), reads=[wzf, xT], writes=[bk])
            P.op("act", lambda e, h=h, bk=bk: e.activation(out=sgm[:, h, :], in_=bk[:, 0:HT], func=AF.Sigmoid, scale=-1.0),
                 reads=[bk], writes=[sgm])
        if want_out:
            for h in range(8):
                bk = P.bank()
                for k in range(8):
                    P.op("pe", lambda e, k=k, h=h, bk=bk: e.matmul(bk[:, 0:HT], wq[:, k, h * 128:(h + 1) * 128], xT[:, k, :],
                                                                  start=(k == 0), stop=(k == 7)), reads=[wq, xT], writes=[bk])
                P.op("act", lambda e, h=h, bk=bk: e.activation(out=sq[:, h, :], in_=bk[:, 0:HT], func=AF.Silu),
                     reads=[bk], writes=[sq])
        for b in range(HNB):
            for nh in range(2):
                bk = P.bank()
                for k in range(8):
                    P.op("pe", lambda e, k=k, b=b, nh=nh, bk=bk: e.matmul(
                        bk[:, 0:512], xT[:, k, b * 128:(b + 1) * 128], wv[:, k, nh * 512:(nh + 1) * 512],
                        start=(k == 0), stop=(k == 7)), reads=[xT, wv], writes=[bk])
                P.op("act", lambda e, b=b, nh=nh, bk=bk: e.copy(out=vt[:, b, nh * 512:(nh + 1) * 512], in_=bk[:, 0:512]),
                     reads=[bk], writes=[vt])
        for h in range(8):
            P.op("act", lambda e, h=h: e.activation(out=gb[:, h, :], in_=sgm[:, h, :], func=AF.Ln, scale=noml[:, h:h + 1], bias=1.0),
                 reads=[sgm, noml], writes=[gb])
        P.op("dve", lambda e: e.tensor_reduce(out=gred[:], in_=gb[:], axis=AX.X, op=ALU.add), reads=[gb], writes=[gred])
        P.op("dve", lambda e: e.tensor_tensor(out=gsum[:], in0=gsum[:], in1=gred[:], op=ALU.add), reads=[gsum, gred], writes=[gsum])
        for h in range(8):
            P.op("dve", lambda e, h=h: e.tensor_tensor_scan(out=rv(bb[:, h, :]), data0=rmask, data1=rv(gb[:, h, :]), initial=0.0,
                                                           op0=ALU.mult, op1=ALU.add), reads=[gb, cm], writes=[bb])
        P.op("act", lambda e: e.activation(out=gb[:], in_=bb[:], func=AF.Exp), reads=[bb], writes=[gb])
        P.op("act", lambda e: e.activation(out=bb[:], in_=bb[:], func=AF.Exp, scale=-1.0), reads=[bb], writes=[bb])
        lastc = 63 if fwd else 0
        for h in range(8):
            P.op("dve", lambda e, h=h: e.scalar_tensor_tensor(out=kh[:, h, :], in0=sgm[:, h, :], scalar=oml[:, h:h + 1],
                                                             in1=bb[:, h, :], op0=ALU.mult, op1=ALU.mult),
                 reads=[sgm, oml, bb], writes=[kh])
            E3 = gb[:, h, :].rearrange("p (c t) -> p c t", t=64)[:, :, lastc:lastc + 1].to_broadcast([128, HNC, 64])
            P.op("pool", lambda e, h=h, E3=E3: e.tensor_tensor(
                out=kbT[:, h, :].rearrange("p (c t) -> p c t", t=64), in0=kh[:, h, :].rearrange("p (c t) -> p c t", t=64),
                in1=E3, op=ALU.mult), reads=[kh, gb], writes=[kbT])
        if want_out:
            P.op("dve", lambda e: e.tensor_tensor(out=qt[:], in0=sq[:], in1=gb[:], op=ALU.mult), reads=[sq, gb], writes=[qt])
        for hp in range(4):
            bk = P.bank()
            bkb = bk[:].bitcast(BF16)
            for hh in range(2):
                h = hp * 2 + hh
                for b in range(HNB):
                    P.op("pe", lambda e, h=h, hh=hh, b=b, bkb=bkb: e.transpose(
                        bkb[:, (hh * HNB + b) * 128:(hh * HNB + b + 1) * 128], kbT[:, h, b * 128:(b + 1) * 128], C.identb[:]),
                        reads=[kbT, C.identb], writes=[bk])
            for hh in range(2):
                h = hp * 2 + hh
                src = bkb[:, hh * HNB * 128:(hh + 1) * HNB * 128].rearrange("p (b k) -> p b k", k=128)
                if hh == 0:
                    P.op("act", lambda e, h=h, src=src: e.copy(out=kbm[:, :, h, :], in_=src), reads=[bk], writes=[kbm])
                else:
                    P.op("dve", lambda e, h=h, src=src: e.tensor_copy(out=kbm[:, :, h, :], in_=src), reads=[bk], writes=[kbm])
        corder = list(range(HNC)) if fwd else list(range(HNC - 1, -1, -1))
        for c in corder:
            b, half = c // 2, c % 2
            hs = slice(half * 64, half * 64 + 64)
            cs_ = slice(c * 64, c * 64 + 64)
            if want_out:
                bs = P.bank()
                for h in range(8):
                    P.op("pe", lambda e, h=h, bs=bs, hs=hs, cs_=cs_: e.matmul(
                        bs[hs, h * 64:(h + 1) * 64], kh[:, h, cs_], qt[:, h, cs_], start=True, stop=True),
                        reads=[kh, qt], writes=[bs])
                sk = smk[smk_i[0] % 2]
                smk_i[0] += 1
                P.op("dve", lambda e, bs=bs, sk=sk, hs=hs: e.tensor_tensor(out=sk[hs, :], in0=bs[hs, 0:512], in1=tri[hs, :], op=ALU.mult),
                     reads=[bs, cm], writes=[sk])
                bos = [P.bank(), P.bank()]
                for h in range(8):
                    bo = bos[h // 4]
                    oc = slice((h % 4) * 128, (h % 4 + 1) * 128)
                    P.op("pe", lambda e, h=h, bo=bo, oc=oc, hs=hs, cs_=cs_: e.matmul(
                        bo[hs, oc], qt[:, h, cs_], Sb[:, h, :], start=True, stop=False), reads=[qt, Sb], writes=[bo])
                    P.op("pe", lambda e, h=h, bo=bo, oc=oc, hs=hs, sk=sk, b=b: e.matmul(
                        bo[hs, oc], sk[hs, h * 64:(h + 1) * 64], vt[hs, b, h * 128:(h + 1) * 128], start=False, stop=True),
                        reads=[sk, vt], writes=[bo])
                for q in range(2):
                    if fwd:
                        P.op("act", lambda e, q=q, bo=bos[q], hs=hs, b=b: e.copy(out=osb[hs, b, q * 512:(q + 1) * 512], in_=bo[hs, 0:512]),
                             reads=[bos[q]], writes=[osb])
                    else:
                        P.op("dve", lambda e, q=q, bo=bos[q], hs=hs, b=b: e.tensor_tensor(
                            out=osb[hs, b, q * 512:(q + 1) * 512], in0=bo[hs, 0:512], in1=of[hs, b, q * 512:(q + 1) * 512], op=ALU.add),
                            reads=[bos[q], of], writes=[osb])
            bds = [P.bank(), P.bank()]
            ecol = c * 64 + lastc
            for h in range(8):
                bd = bds[h // 4]
                oc = slice((h % 4) * 128, (h % 4 + 1) * 128)
                P.op("pe", lambda e, h=h, bd=bd, oc=oc, hs=hs, b=b: e.matmul(
                    bd[:, oc], kbm[hs, b, h, :], vt[hs, b, h * 128:(h + 1) * 128], start=True, stop=True),
                    reads=[kbm, vt], writes=[bd])
            for h in range(8):
                bd = bds[h // 4]
                oc = slice((h % 4) * 128, (h % 4 + 1) * 128)
                P.op("dve", lambda e, h=h, bd=bd, oc=oc, ecol=ecol: e.scalar_tensor_tensor(
                    out=S[:, h, :], in0=S[:, h, :], scalar=gb[:, h, ecol:ecol + 1], in1=bd[:, oc], op0=ALU.mult, op1=ALU.add),
                    reads=[S, gb, bd], writes=[S])
            if want_out:
                P.op("act", lambda e: e.copy(out=Sb[:], in_=S[:]), reads=[S], writes=[Sb])
        if want_out:
            P.dma("sp", lambda e, r0=r0: e.dma_start(out=o_d[r0:r0 + HT, :].rearrange("(b p) d -> p b d", p=128), in_=osb[:]),
                  reads=[osb], writes=[o_d])
    if not want_out:
        P.dma("sp", lambda e: e.dma_start(out=sout_d[:, :, :], in_=S[:]), reads=[S], writes=[sout_d])
        P.op("act", lambda e: e.activation(out=gsum[:], in_=gsum[:], func=AF.Exp), reads=[gsum], writes=[gsum])
        P.dma("sp", lambda e: e.dma_start(out=aout_d[:, :], in_=gsum[:]), reads=[gsum], writes=[aout_d])


def emit_state_combine(P, d, sin_d, sm_d, S):
    BA = P.sbuf("sc_BA", [128, 4, 8 * 128 + 8], F32)
    smk = P.sbuf("sc_m", [128, 8], F32)
    tmp = P.sbuf("sc_tmp", [128, 8, 128], F32)
    P.dma("sp", lambda e: e.dma_start(out=BA[:], in_=sin_d[:, :, :].rearrange("j p c -> p j c")), reads=[sin_d], writes=[BA])
    P.dma("sp", lambda e: e.dma_start(out=smk[:], in_=sm_d[:, :]), reads=[sm_d], writes=[smk])
    P.op("dve", lambda e: e.memset(S[:].rearrange("p a b -> p (a b)"), 0.0), writes=[S])
    order = [0, 1, 2, 3] if d == 0 else [3, 2, 1, 0]
    for jp in order:
        mcol = smk[:, d * 4 + jp:d * 4 + jp + 1]
        for h in range(8):
            Bh = BA[:, jp, h * 128:(h + 1) * 128]
            Ah = BA[:, jp, 1024 + h:1024 + h + 1]
            P.op("dve", lambda e, h=h, Bh=Bh, Ah=Ah: e.scalar_tensor_tensor(
                out=tmp[:, h, :], in0=S[:, h, :], scalar=Ah, in1=Bh, op0=ALU.mult, op1=ALU.add), reads=[S, BA], writes=[tmp])
            P.op("dve", lambda e, h=h: e.tensor_tensor(out=tmp[:, h, :], in0=tmp[:, h, :], in1=S[:, h, :], op=ALU.subtract),
                 reads=[tmp, S], writes=[tmp])
            P.op("dve", lambda e, h=h, mcol=mcol: e.scalar_tensor_tensor(
                out=S[:, h, :], in0=tmp[:, h, :], scalar=mcol, in1=S[:, h, :], op0=ALU.mult, op1=ALU.add),
                reads=[tmp, smk, S], writes=[S])


def emit_hgrn_post(P, C, W, l, j, x_d, o_d, xm_d):
    wgt = P.sbuf("hp_wg", [128, 8, 1024], BF16)
    wo = P.sbuf("hp_wo", [128, 8, 1024], BF16)
    gbc = P.sbuf("hp_gbc", [128, 1024], F32)
    bbc = P.sbuf("hp_bbc", [128, 1024], F32)
    ng = P.sbuf("hp_ng", [128, 128], F32)
    win_v = W["hgrn_w_in"][j].rearrange("(k p) n -> p k n", p=128)
    wo_v = W["hgrn_w_o"][j].rearrange("(k p) n -> p k n", p=128)
    for k0 in range(0, 8, 4):
        load_w(P, wgt, wgt[:, k0:k0 + 4, :], W["hgrn_w_in"], win_v[:, k0:k0 + 4, 4096:5120])
        load_w(P, wo, wo[:, k0:k0 + 4, :], W["hgrn_w_o"], wo_v[:, k0:k0 + 4, :])
    P.dma("sp", lambda e: e.dma_start(out=gbc[:], in_=W["ln_mix_g"][l:l + 1, :].partition_broadcast(128)), reads=[W["ln_mix_g"]], writes=[gbc])
    P.dma("sp", lambda e: e.dma_start(out=bbc[:], in_=W["ln_mix_b"][l:l + 1, :].partition_broadcast(128)), reads=[W["ln_mix_b"]], writes=[bbc])
    P.dma("sp", lambda e: e.dma_start(out=ng[:], in_=W["hgrn_norm_g"][j:j + 1, :].partition_broadcast(128)), reads=[W["hgrn_norm_g"]], writes=[ng])
    TB = 512
    nb = 4
    xt = [P.sbuf("hp_xt%d" % i, [128, nb, 1024], F32) for i in range(2)]
    ot = [P.sbuf("hp_ot%d" % i, [128, nb, 1024], F32) for i in range(2)]
    xT = P.sbuf("hp_xT", [128, 8, TB], BF16)
    sgt = [P.sbuf("hp_sg%d" % i, [128, 1024], F32) for i in range(2)]
    t1 = [P.sbuf("hp_t1%d" % i, [128, 1024], F32) for i in range(2)]
    ss = P.sbuf("hp_ss", [128, 8], F32)
    og = P.sbuf("hp_og", [128, 1024], BF16)
    ogT = P.sbuf("hp_ogT", [128, 8, 128], BF16)
    xo = [P.sbuf("hp_xo%d" % i, [128, 1024], F32) for i in range(2)]
    pend = None
    cnt = 0
    for t in range(NTOK // TB):
        r0 = t * TB
        x_t, o_t = xt[t % 2], ot[t % 2]
        P.dma("sp", lambda e, x_t=x_t, r0=r0: e.dma_start(out=x_t[:], in_=x_d[r0:r0 + TB, :].rearrange("(b p) d -> p b d", p=128)),
              reads=[x_d], writes=[x_t])
        P.dma("sp", lambda e, o_t=o_t, r0=r0: e.dma_start(out=o_t[:], in_=o_d[r0:r0 + TB, :].rearrange("(b p) d -> p b d", p=128)),
              reads=[o_d], writes=[o_t])
        transpose_blocks(P, C, x_t, lambda b, k, x_t=x_t: x_t[:, b, k * 128:(k + 1) * 128], nb, 8, xT, lambda k: xT[:, k, :], C.identf)
        for b in range(nb):
            sg = sgt[cnt % 2]
            tt = t1[cnt % 2]
            cnt += 1
            for nh in range(2):
                bk = P.bank()
                for k in range(8):
                    P.op("pe", lambda e, k=k, b=b, nh=nh, bk=bk: e.matmul(
                        bk[:, 0:512], xT[:, k, b * 128:(b + 1) * 128], wgt[:, k, nh * 512:(nh + 1) * 512],
                        start=(k == 0), stop=(k == 7)), reads=[xT, wgt], writes=[bk])
                P.op("act", lambda e, nh=nh, bk=bk, sg=sg: e.activation(out=sg[:, nh * 512:(nh + 1) * 512], in_=bk[:, 0:512], func=AF.Silu),
                     reads=[bk], writes=[sg])
            ob = o_t[:, b, :]
            P.op("pool", lambda e, ob=ob, tt=tt: e.tensor_tensor(out=tt[:], in0=ob, in1=ob, op=ALU.mult), reads=[o_t], writes=[tt])
            P.op("dve", lambda e, tt=tt: e.tensor_reduce(out=ss[:], in_=tt[:].rearrange("p (h v) -> p h v", v=128), axis=AX.X, op=ALU.add),
                 reads=[tt], writes=[ss])
            P.op("dve", lambda e: e.tensor_scalar(out=ss[:], in0=ss[:], scalar1=1.0 / 128, scalar2=EPS, op0=ALU.mult, op1=ALU.add),
                 reads=[ss], writes=[ss])
            P.op("pool", lambda e: e.tensor_tensor(out=ss[:], in0=ss[:], in1=C.mhalf[:, 0:1].to_broadcast([128, 8]), op=ALU.pow),
                 reads=[ss, C.mhalf], writes=[ss])
            P.op("dve", lambda e, ob=ob, tt=tt: e.tensor_tensor(
                out=tt[:].rearrange("p (h v) -> p h v", v=128), in0=ob.rearrange("p (h v) -> p h v", v=128),
                in1=ss[:].rearrange("p (h o) -> p h o", o=1).to_broadcast([128, 8, 128]), op=ALU.mult), reads=[o_t, ss], writes=[tt])
            P.op("pool", lambda e, tt=tt: e.tensor_tensor(
                out=tt[:].rearrange("p (h v) -> p h v", v=128), in0=tt[:].rearrange("p (h v) -> p h v", v=128),
                in1=ng[:].rearrange("p (o v) -> p o v", o=1).to_broadcast([128, 8, 128]), op=ALU.mult), reads=[tt, ng], writes=[tt])
            P.op("dve", lambda e, tt=tt, sg=sg: e.tensor_tensor(out=og[:], in0=tt[:], in1=sg[:], op=ALU.mult), reads=[tt, sg], writes=[og])
            bk = P.bank()
            bkb = bk[:].bitcast(BF16)
            for k in range(8):
                P.op("pe", lambda e, k=k, bkb=bkb: e.transpose(bkb[:, k * 128:(k + 1) * 128], og[:, k * 128:(k + 1) * 128], C.identb[:]),
                     reads=[og, C.identb], writes=[bk])
            P.op("act", lambda e, bkb=bkb: e.copy(out=ogT[:].rearrange("p k t -> p (k t)"), in_=bkb[:, 0:1024]), reads=[bk], writes=[ogT])
            bks = []
            for nh in range(2):
                bk = P.bank()
                for k in range(8):
                    P.op("pe", lambda e, k=k, nh=nh, bk=bk: e.matmul(
                        bk[:, 0:512], ogT[:, k, :], wo[:, k, nh * 512:(nh + 1) * 512], start=(k == 0), stop=(k == 7)),
                        reads=[ogT, wo], writes=[bk])
                bks.append(bk)
            s = ln_part_a(P, C, bks, x_t, x_t[:, b, :])
            if pend is not None:
                prow, ps, pi = pend
                xo_p = xo[pi % 2]
                ln_part_b(P, C, ps, gbc, bbc, xo_p, xo_p[:])
                P.dma("sp", lambda e, xo_p=xo_p, prow=prow: e.dma_start(out=xm_d[prow:prow + 128, :], in_=xo_p[:]), reads=[xo_p], writes=[xm_d])
            pend = (r0 + b * 128, s, cnt)
    prow, ps, pi = pend
    xo_p = xo[pi % 2]
    ln_part_b(P, C, ps, gbc, bbc, xo_p, xo_p[:])
    P.dma("sp", lambda e: e.dma_start(out=xm_d[prow:prow + 128, :], in_=xo_p[:]), reads=[xo_p], writes=[xm_d])


HG_W1 = ["hgrn_w_in", "hgrn_lb_logits"]
HG_W2 = ["hgrn_w_in", "hgrn_lb_logits", "hgrn_norm_g", "hgrn_w_o", "ln_mix_g", "ln_mix_b", "ffn_w_in", "ffn_w_out",
         "ln_ffn_g", "ln_ffn_b", "ple_w_gate", "ple_w_proj"]


def build_hgrn_pass1(l):
    nc = bass.Bass("TRN2", target_bir_lowering=False)
    P = Prog(nc)
    W = declare_weights(P, HG_W1)
    ident_d = P.dram("ident", [128, 128], F32, kind="ExternalInput")
    x_d = P.dram("x", [NTOK, D], F32, kind="ExternalInput")
    cm_d = P.dram("cm", [3, 128, 512], F32, kind="ExternalInput")
    outs = {}
    for d in range(2):
        outs["sB%d" % d] = P.dram("sB%d" % d, [128, 8, 128], F32, kind="ExternalOutput")
        outs["sA%d" % d] = P.dram("sA%d" % d, [128, 8], F32, kind="ExternalOutput")
    C = Common(P, ident_d)
    for d in range(2):
        with P.phase():
            emit_hgrn_sweep(P, C, W, 0, 0, d, False, x_d, cm_d, None, None, None, outs["sB%d" % d], outs["sA%d" % d], l_real=l)
    print("hgrn pass1 ops:", P.nops)
    return nc


def build_hgrn_pass2(l):
    nc = bass.Bass("TRN2", target_bir_lowering=False)
    P = Prog(nc)
    W = declare_weights(P, HG_W2)
    ident_d = P.dram("ident", [128, 128], F32, kind="ExternalInput")
    x_d = P.dram("x", [NTOK, D], F32, kind="ExternalInput")
    cm_d = P.dram("cm", [3, 128, 512], F32, kind="ExternalInput")
    p_d = P.dram("p", [NTOK, PLE], F32, kind="ExternalInput")
    ba = [P.dram("BA%d" % d, [4, 128, 1032], F32, kind="ExternalInput") for d in range(2)]
    sm_d = P.dram("smask", [128, 8], F32, kind="ExternalInput")
    o_d = P.dram("o_scr", [NTOK, D], F32)
    xm_d = P.dram("xm_scr", [NTOK, D], F32, kind="ExternalOutput" if DEBUG else "Internal")
    out_d = P.dram("out", [NTOK, D], F32, kind="ExternalOutput")
    C = Common(P, ident_d)
    for d in range(2):
        with P.phase():
            emit_hgrn_sweep(P, C, W, 0, 0, d, True, x_d, cm_d, o_d, ba[d], sm_d, None, None, l_real=l)
    with P.phase():
        emit_hgrn_post(P, C, W, 0, 0, x_d, o_d, xm_d)
    with P.phase():
        emit_stage_c(P, C, W, 0, xm_d, p_d, out_d)
    print("hgrn pass2 ops:", P.nops)
    return nc


def hgrn_consts():
    cm = np.zeros((3, 128, 512), np.float32)
    rm = np.ones(512, np.float32)
    rm[::64] = 0.0
    cm[0] = rm[None, :]
    s = (np.arange(128) % 64)[:, None]
    t = np.arange(64)[None, :]
    cm[1] = np.tile((s <= t).astype(np.float32), (1, 8))
    cm[2] = np.tile((s >= t).astype(np.float32), (1, 8))
    return cm


def split_tokens(xfull):
    return [np.ascontiguousarray(xfull[c // 4, (c % 4) * NTOK:(c % 4 + 1) * NTOK]) for c in range(NCORE)]


_PROGS = {}


def get_prog(kind, l):
    key = (kind, l)
    if key not in _PROGS:
        _PROGS[key] = {"attn": build_attn_layer, "h1": build_hgrn_pass1, "h2": build_hgrn_pass2}[kind](l)
    return _PROGS[key]


def run_hgrn_layer(xfull, p_l, Wnp, l):
    ident = np.eye(128, dtype=np.float32)
    cm = hgrn_consts()
    xs = split_tokens(xfull)
    maps = []
    for c in range(NCORE):
        m = {"ident": ident, "x": xs[c], "cm": cm}
        m.update(slice_weights(Wnp, HG_W1, l))
        maps.append(m)
    r1 = run_bass_kernel_spmd(get_prog("h1", l), maps, core_ids=list(range(NCORE)))
    maps = []
    w2 = slice_weights(Wnp, HG_W2, l)
    ps = split_tokens(p_l)
    for c in range(NCORE):
        b, jpos = c // 4, c % 4
        m = {"ident": ident, "x": xs[c], "cm": cm, "p": ps[c]}
        for d in range(2):
            BA = np.zeros((4, 128, 1032), np.float32)
            for jp in range(4):
                rr = r1.results[b * 4 + jp]
                BA[jp, :, 0:1024] = rr["sB%d" % d].reshape(128, 1024)
                BA[jp, :, 1024:1032] = rr["sA%d" % d]
            m["BA%d" % d] = BA
        sm = np.zeros((128, 8), np.float32)
        for jp in range(4):
            sm[:, jp] = 1.0 if jp < jpos else 0.0
            sm[:, 4 + jp] = 1.0 if jp > jpos else 0.0
        m["smask"] = sm
        m.update(w2)
        maps.append(m)
    r2 = run_bass_kernel_spmd(get_prog("h2", l), maps, core_ids=list(range(NCORE)))
    return gather_tokens(r2), r1, r2


def kernel(**inputs):
    Wnp = {n: np.ascontiguousarray(np.asarray(inputs[n], dtype=np.float32)) for n in WNAMES}
    x = np.ascontiguousarray(np.asarray(inputs["x"], dtype=np.float32))
    p = np.asarray(inputs["p"], dtype=np.float32)
    for l in range(4):
        if l % 2 == 0:
            res = run_bass_kernel_spmd(get_prog("attn", l), attn_in_maps(x, p[l], Wnp, l), core_ids=list(range(NCORE)))
            x = gather_tokens(res)
        else:
            x, _, _ = run_hgrn_layer(x, p[l], Wnp, l)
    return x
```
